# Optimizing a Trainium2 kernel written in Bass

```python
import jax, jax.numpy as jnp
from jax import lax
import numpy as np

D_MODEL = 4096
BATCH = 8
SEQ = 2048
DEPTH = 2
DEC_BATCH = 4
DEC_SEQ = 4096
PAST_LEN = 128

CHUNK = 128
D_A = D_MODEL // 2
A_HEAD_DIM = 128
A_HEADS = D_A // A_HEAD_DIM
D_B = D_MODEL // 2
B_GROUPS = 8
B_GROUP_DIM = D_B // B_GROUPS
D_IN = 3 * D_A + 2 * D_B + 2 * D_MODEL
SPLITS = (D_A, 2 * D_A, 3 * D_A, 3 * D_A + D_B, 3 * D_A + 2 * D_B, 3 * D_A + 2 * D_B + D_MODEL)
EPS = 1e-6

kernel_name = "gated_gmlp_fnet_hybrid_encoder"


def rms_norm(x, g):
    xf = x.astype(jnp.float32)
    y = xf * lax.rsqrt(jnp.mean(xf * xf, axis=-1, keepdims=True) + EPS)
    return (y * g.astype(jnp.float32)).astype(x.dtype)


def layer_norm(x, g, b):
    xf = x.astype(jnp.float32)
    mu = jnp.mean(xf, axis=-1, keepdims=True)
    xc = xf - mu
    var = jnp.mean(xc * xc, axis=-1, keepdims=True)
    y = xc * lax.rsqrt(var + EPS)
    return (y * g.astype(jnp.float32) + b.astype(jnp.float32)).astype(x.dtype)


def spatial_gating_branch(u, v, z, ln_g, ln_b, w_s, b_s):
    bsz, s, _ = u.shape
    u = jax.nn.gelu(u)
    v = layer_norm(jax.nn.gelu(v), ln_g, ln_b)
    vc = v.reshape(bsz, s // CHUNK, CHUNK, A_HEADS, A_HEAD_DIM)
    mixed = jnp.einsum('hpq,bcqhd->bcphd', w_s, vc) + jnp.transpose(b_s)[None, None, :, :, None]
    mixed = mixed.reshape(bsz, s, D_A)
    return u * mixed * jax.nn.silu(z)


def fourier_branch(xb, z):
    bsz, s, _ = xb.shape
    xg = xb.astype(jnp.float32).reshape(bsz, s, B_GROUPS, B_GROUP_DIM)
    f = jnp.fft.fft2(xg, axes=(1, 3), norm='ortho').real
    f = f.reshape(bsz, s, D_B).astype(xb.dtype)
    return f * jax.nn.silu(z)


def mixer_layer(x, norm_g, w_in, sgu_ln_g, sgu_ln_b, w_spatial, b_spatial, w_a, w_b, b_gate, w_out):
    h = rms_norm(x, norm_g)
    p = jnp.einsum('bsd,de->bse', h, w_in)
    u, v, z_a, xb, z_b, g_a, g_b = jnp.split(p, SPLITS, axis=-1)
    y_a = jnp.einsum('bsc,cd->bsd', spatial_gating_branch(u, v, z_a, sgu_ln_g, sgu_ln_b, w_spatial, b_spatial), w_a)
    y_b = jnp.einsum('bsc,cd->bsd', fourier_branch(xb, z_b), w_b)
    m = jax.nn.sigmoid(g_a + b_gate[0]) * y_a + jax.nn.sigmoid(g_b + b_gate[1]) * y_b
    return x + jnp.einsum('bsd,de->bse', m, w_out)


def trunk(x, norm_g, w_in, sgu_ln_g, sgu_ln_b, w_spatial, b_spatial, w_a, w_b, b_gate, w_out, final_g):
    for l in range(DEPTH):
        x = mixer_layer(x, norm_g[l], w_in[l], sgu_ln_g[l], sgu_ln_b[l], w_spatial[l], b_spatial[l],
                        w_a[l], w_b[l], b_gate[l], w_out[l])
    return rms_norm(x, final_g)


def setup_inputs(seed: int = 0) -> dict:
    key = jax.random.key(seed)
    ks = jax.random.split(key, 14)
    f32 = jnp.float32
    x_prompt = jax.random.normal(ks[0], (BATCH, SEQ, D_MODEL), f32)
    x_sample = jax.random.normal(ks[1], (DEC_BATCH, DEC_SEQ, D_MODEL), f32)
    norm_g = 1.0 + 0.02 * jax.random.normal(ks[2], (DEPTH, D_MODEL), f32)
    w_in = jax.random.normal(ks[3], (DEPTH, D_MODEL, D_IN), f32) * D_MODEL ** -0.5
    sgu_ln_g = 1.0 + 0.02 * jax.random.normal(ks[4], (DEPTH, D_A), f32)
    sgu_ln_b = 0.02 * jax.random.normal(ks[5], (DEPTH, D_A), f32)
    w_spatial = jax.random.normal(ks[6], (DEPTH, A_HEADS, CHUNK, CHUNK), f32) * CHUNK ** -0.5
    b_spatial = 1.0 + 0.02 * jax.random.normal(ks[7], (DEPTH, A_HEADS, CHUNK), f32)
    w_a = jax.random.normal(ks[8], (DEPTH, D_A, D_MODEL), f32) * D_A ** -0.5
    w_b = jax.random.normal(ks[9], (DEPTH, D_B, D_MODEL), f32) * D_B ** -0.5
    b_gate = 0.02 * jax.random.normal(ks[10], (DEPTH, 2, D_MODEL), f32)
    w_out = jax.random.normal(ks[11], (DEPTH, D_MODEL, D_MODEL), f32) * D_MODEL ** -0.5
    final_g = 1.0 + 0.02 * jax.random.normal(ks[12], (D_MODEL,), f32)
    return {"x_prompt": x_prompt, "x_sample": x_sample, "norm_g": norm_g, "w_in": w_in,
            "sgu_ln_g": sgu_ln_g, "sgu_ln_b": sgu_ln_b, "w_spatial": w_spatial, "b_spatial": b_spatial,
            "w_a": w_a, "w_b": w_b, "b_gate": b_gate, "w_out": w_out, "final_g": final_g}


def reference(x_prompt, x_sample, norm_g, w_in, sgu_ln_g, sgu_ln_b, w_spatial, b_spatial, w_a, w_b, b_gate, w_out, final_g):
    y_prompt = trunk(x_prompt, norm_g, w_in, sgu_ln_g, sgu_ln_b, w_spatial, b_spatial, w_a, w_b, b_gate, w_out, final_g)
    y_sample = trunk(x_sample, norm_g, w_in, sgu_ln_g, sgu_ln_b, w_spatial, b_spatial, w_a, w_b, b_gate, w_out, final_g)
    return (y_prompt, y_sample)
```

```python
import numpy as np
import ml_dtypes
import concourse.bass as bass
import concourse.mybir as mybir
from concourse.bass_utils import run_bass_kernel_spmd

F32 = mybir.dt.float32
BF16 = mybir.dt.bfloat16
AF = mybir.ActivationFunctionType
ALU = mybir.AluOpType

D = 4096
D_A = 2048
D_IN = 18432
EPS = 1e-6
NS = 8


class Sem:
    def __init__(self, h, name):
        self.h = h
        self.name = name
        self.count = 0


class Region:
    def __init__(self, name):
        self.name = name
        self.writers = {}
        self.readers = {}


class Buf:
    def __init__(self, name, regions):
        self.name = name
        self.regions = regions
        self.lsem = None
        self.ssem = None


class Eng:
    def __init__(self, name, sem):
        self.name = name
        self.sem = sem
        self.seen = {}
        self.prog = []


class Sched:
    def __init__(self, nc, sem_alloc):
        self.nc = nc
        self.sem_alloc = sem_alloc
        self.engs = {}
        for n in ("pe", "act", "dve", "pool", "sp"):
            self.engs[n] = Eng(n, sem_alloc("e_" + n))

    def _deps(self, reads, writes):
        deps = {}

        def add(d):
            for s, v in d.items():
                if deps.get(s, 0) < v:
                    deps[s] = v

        for b in reads:
            for r in b.regions:
                add(r.writers)
        for b in writes:
            for r in b.regions:
                add(r.writers)
                add(r.readers)
        return deps

    def _commit(self, reads, writes, sem, val):
        wr = set()
        for b in writes:
            for r in b.regions:
                wr.add(id(r))
                if r.readers:
                    r.writers = {}
                    r.readers = {}
                r.writers[sem] = max(r.writers.get(sem, 0), val)
        for b in reads:
            for r in b.regions:
                if id(r) in wr:
                    continue
                r.readers[sem] = max(r.readers.get(sem, 0), val)

    def _waits(self, eng, deps):
        waits = []
        for s, v in deps.items():
            if eng.seen.get(s, 0) < v:
                eng.seen[s] = v
                waits.append((s, v))
        return waits

    def op(self, engname, fn, reads=(), writes=(), skip_own=False):
        eng = self.engs[engname]
        deps = self._deps(reads, writes)
        if skip_own:
            deps.pop(eng.sem, None)
        waits = self._waits(eng, deps)
        eng.sem.count += 1
        val = eng.sem.count
        eng.seen[eng.sem] = max(eng.seen.get(eng.sem, 0), 0)
        eng.prog.append((waits, fn, eng.sem, 1))
        self._commit(reads, writes, eng.sem, val)

    def dma(self, qname, out, in_, reads=(), writes=(), sem=None, **kw):
        eng = self.engs[qname]
        deps = self._deps(reads, writes)
        waits = self._waits(eng, deps)
        sem.count += 16
        val = sem.count

        def fn(e, out=out, in_=in_, kw=kw):
            return e.dma_start(out=out, in_=in_, **kw)

        eng.prog.append((waits, fn, sem, 16))
        self._commit(reads, writes, sem, val)

    def final_wait(self, qname, bufs):
        eng = self.engs[qname]
        deps = self._deps((), bufs)
        waits = self._waits(eng, deps)
        eng.prog.append((waits, None, None, 0))

    @staticmethod
    def run(eng, e):
        for waits, fn, sem, amt in eng.prog:
            for s, v in waits:
                e.wait_ge(s.h, v)
            if fn is not None:
                ins = fn(e)
                ins.then_inc(sem.h, amt)


def build_program(TOK, DEPTH):
    NT = TOK // 512
    NCH = TOK // 128
    NHALF = TOK // 256
    nc = bass.Bass("TRN2", target_bir_lowering=False)

    def din(name, shape, dt=F32):
        return nc.dram_tensor(name, list(shape), dt, kind="ExternalInput").ap()

    def dscr(name, shape, dt):
        return nc.dram_tensor(name, list(shape), dt, kind="Internal").ap()

    x_in = din("x", [TOK, D])
    w_in = din("w_in", [DEPTH, D, D_IN])
    w_a = din("w_a", [DEPTH, D_A, D])
    w_b = din("w_b", [DEPTH, D_A, D])
    w_out = din("w_out", [DEPTH, D, D])
    norm_g = din("norm_g", [DEPTH, D])
    ln_g = din("sgu_ln_g", [DEPTH, D_A])
    ln_b = din("sgu_ln_b", [DEPTH, D_A])
    w_sp = din("w_spatial", [DEPTH, 16, 128, 128])
    b_sp = din("b_spatial", [DEPTH, 16 * 128])
    b_gate = din("b_gate", [DEPTH, 2, D])
    final_g = din("final_g", [D])
    cs256_d = din("cs256", [128, 2, 512], BF16)
    dft_d = din("dft", [NHALF, 128, NCH, 2, 256], BF16)
    identb_d = din("identb", [128, 128], BF16)
    identf_d = din("identf", [128, 128], F32)
    ones_d = din("ones", [1, 128], F32)
    y_out = nc.dram_tensor("y", [TOK, D], F32, kind="ExternalOutput").ap()

    hT_s = dscr("hT_s", [NT, 128, 32, 512], BF16)
    AB_s = dscr("AB_s", [16, 128, NCH, 256], BF16)
    y_s = dscr("y_s", [TOK, D_A], BF16)
    xa_s = dscr("xa_s", [TOK, D], F32)
    xb_s = dscr("xb_s", [TOK, D], F32)

    import contextlib

    with contextlib.ExitStack() as st:
        PAGE = 16384
        NPAGE = 12
        TAIL = 16128
        big = st.enter_context(nc.sbuf_tensor("big", [128, (NPAGE * PAGE + TAIL) // 2], BF16))
        banks = [st.enter_context(nc.psum_tensor(f"ps{i}", [128, 512], F32)) for i in range(8)]
        sem_list = []

        def sem_alloc(name):
            h = st.enter_context(nc.semaphore(name))
            s = Sem(h, name)
            sem_list.append(s)
            return s

        S = Sched(nc, sem_alloc)

        def view(off, nbytes, dt):
            v = big[:, off // 2:(off + nbytes) // 2]
            if dt == F32:
                v = v.bitcast(F32)
            return v

        pages = [Region(f"page{i}") for i in range(NPAGE)]

        def mkbuf(name, regions, load=False, store=False):
            b = Buf(name, regions)
            if load:
                b.lsem = sem_alloc("l_" + name)
            if store:
                b.ssem = sem_alloc("s_" + name)
            return b

        SLOT = 8192
        slot_regs = [Region(f"slot{i}") for i in range(NS)]
        slot_sems = [sem_alloc(f"l_slot{i}") for i in range(NS)]
        slot_ctr = [0]

        def next_slot(double=False):
            if double and slot_ctr[0] % 2 == 1:
                slot_ctr[0] += 1
            i = slot_ctr[0] % NS
            n = 2 if double else 1
            slot_ctr[0] += n
            b = Buf(f"slot{i}x{n}", slot_regs[i:i + n])
            b.lsem = slot_sems[i]
            b.flat = view(i * SLOT, n * SLOT, BF16)
            return b

        HT = mkbuf("HT", [pages[4], pages[5]], load=True, store=True)
        HT.ap = view(4 * PAGE, 2 * PAGE, BF16).rearrange("p (a b) -> p a b", b=512)

        XIN = []
        for i in range(2):
            b = mkbuf(f"xin{i}", [pages[6 + i]], load=True, store=True)
            b.ap = view((6 + i) * PAGE, PAGE, F32)
            XIN.append(b)
        r_p11a, r_p11b = Region("p11a"), Region("p11b")
        XBT = mkbuf("xbt", [pages[8]])
        XBT.ap = view(8 * PAGE, PAGE, BF16).rearrange("p (a b) -> p a b", b=512)
        GV = mkbuf("gv", [pages[8], pages[9]])
        GV.ap = view(8 * PAGE, 2 * PAGE, F32).rearrange("p (a b) -> p a b", b=2048)
        YB = mkbuf("yb", [pages[10]], store=True)
        YB.ap = view(10 * PAGE, PAGE, BF16).rearrange("p (a b) -> p a b", b=2048)
        XS = mkbuf("xs", [r_p11a], store=True)
        XS.ap = view(11 * PAGE, 8192, BF16)
        ABT = mkbuf("abt", [r_p11b], store=True)
        ABT.ap = view(11 * PAGE + 8192, 8192, BF16)

        FB = mkbuf("fb", [pages[6]])
        FB.ap = view(6 * PAGE, PAGE, BF16).rearrange("p (a b) -> p a b", b=512)
        NCL = NCH // 2
        SLP = {}
        for nm, pg in (("p7", 7), ("p8", 8), ("p10", 10)):
            b = mkbuf("slab_" + nm, [pages[pg]], load=True)
            b.ap = view(pg * PAGE, NCL * 2 * 256 * 2, BF16).rearrange("p (a c b) -> p a c b", c=2, b=256)
            SLP[nm] = b
        MT = mkbuf("mt", [pages[7], pages[8]])
        MT.ap = view(7 * PAGE, 2 * PAGE, BF16).rearrange("p (a b) -> p a b", b=512)
        YT = mkbuf("yt", [pages[9]], load=True)
        YT.ap = view(9 * PAGE, PAGE, BF16).rearrange("p (a b) -> p a b", b=2048)
        XR, OT = [], []
        p9 = [Region(f"p9_{i}") for i in range(4)]
        YT.regions = p9
        for i in range(2):
            b = mkbuf(f"xr{i}", [p9[i]], load=True)
            b.ap = view(9 * PAGE + i * 4096, 4096, F32).rearrange("p (a b) -> p a b", b=256)
            XR.append(b)
            b = mkbuf(f"ot{i}", [p9[2 + i]], store=True)
            b.ap = view(9 * PAGE + 8192 + i * 4096, 4096, F32).rearrange("p (a b) -> p a b", b=256)
            OT.append(b)
        AT = mkbuf("at", [pages[10]])
        AT.ap = view(10 * PAGE, PAGE, BF16).rearrange("p (a b) -> p a b", b=512)
        TMP = []
        for i in range(8):
            b = mkbuf(f"tmp{i}", [r_p11a if i < 4 else r_p11b])
            b.ap = view(11 * PAGE + i * 2048, 2048, F32)
            TMP.append(b)
        for i in range(8):
            TMP[i].regions = [Region(f"tmpr{i}")]
        XS.regions = [TMP[i].regions[0] for i in range(4)]
        ABT.regions = [TMP[i].regions[0] for i in range(4, 8)]

        toff = [NPAGE * PAGE]

        def tail(name, nbytes, dt, load=True):
            b = mkbuf(name, [Region(name)], load=load)
            b.ap = view(toff[0], nbytes, dt)
            toff[0] += nbytes
            assert toff[0] <= NPAGE * PAGE + TAIL, (name, toff[0])
            return b

        CS256 = tail("cs256", 2048, BF16)
        CS256.ap = CS256.ap.rearrange("p (a b) -> p a b", b=512)
        WSTB = tail("wstb", 4096, BF16, load=False)
        WSTB.ap = WSTB.ap.rearrange("p (a b) -> p a b", b=128)
        B2 = tail("b2", 8192, F32, load=False)
        B2.ap = B2.ap.rearrange("p (a b) -> p a b", b=128)
        IDB = tail("identb", 256, BF16)
        GT = tail("gt", 128, F32)
        GCOL = tail("gcol", 64, F32)
        BGA = tail("bga", 128, F32)
        BGB = tail("bgb", 128, F32)
        STAT = tail("stat", 256, F32, load=False)
        IDF = mkbuf("identf", [pages[8]], load=True)
        IDF.ap = view(8 * PAGE, 512, F32)
        ONES = mkbuf("ones", [pages[8]], load=True)
        ONES.ap = view(8 * PAGE + 512, 512, F32)

        def stat(c0, c1):
            return STAT.ap[:, c0:c1]

        SN = []
        for i in range(2):
            b = Buf(f"sn{i}", [Region(f"sn{i}")])
            b.ap = STAT.ap[:, 2 * i:2 * i + 2]
            SN.append(b)
        SLNS = [Buf(f"sln{i}", [Region(f"sln{i}")]) for i in range(2)]
        SZ2 = Buf("sz2", p9)
        SZ2.ap = view(9 * PAGE, PAGE, F32).rearrange("p (a b) -> p a b", b=256)

        PS = []
        for i in range(8):
            b = mkbuf(f"bank{i}", [Region(f"bank{i}")])
            b.ap = banks[i][:]
            b.apb = banks[i][:].bitcast(BF16)
            PS.append(b)
        ps_ctr = [0]

        def next_bank():
            b = PS[ps_ctr[0] % 8]
            ps_ctr[0] += 1
            return b

        def dram_buf(name):
            return Buf(name, [Region(name)])

        hT_r, AB_r, y_r, xa_r, xb_r, yout_r = (dram_buf(n) for n in ("hT_s", "AB_s", "y_s", "xa_s", "xb_s", "yout"))

        def mm_unit(bank, mms, reads, cont=False):
            def fn(e, mms=mms):
                ins = None
                for (o, l, r, a, b) in mms:
                    ins = e.matmul(o, l, r, start=a, stop=b)
                return ins

            S.op("pe", fn, reads=reads, writes=[bank], skip_own=cont)

        def load_q(q, idx, ncols, double=False):
            sl = next_slot(double)
            if hasattr(q, "slots"):
                q = q.slots[idx]
            S.dma("sp", sl.flat, q.ap[idx], reads=[q], writes=[sl], sem=sl.lsem)
            sl.ap = sl.flat.rearrange("p (a b) -> p a b", b=ncols)
            return sl

        def run_groups(groups):
            n = len(groups)
            loaded = [None] * n
            c = 0
            for i in range(n):
                while c < n and c - groups[c][2] <= i:
                    loaded[c] = groups[c][0]()
                    c += 1
                groups[i][1](loaded[i])
                loaded[i] = None

        def wcols(wl, c0, n):
            return wl[:, c0:c0 + n].rearrange("(dc p) e -> p dc e", p=128)

        alt = [0]

        def evac_copy(out_ap, in_ap, reads, writes):
            alt[0] ^= 1
            if alt[0]:
                S.op("dve", lambda e: e.tensor_copy(out=out_ap, in_=in_ap), reads=reads, writes=writes)
            else:
                S.op("act", lambda e: e.copy(out=out_ap, in_=in_ap), reads=reads, writes=writes)

        S.dma("sp", CS256.ap, cs256_d, writes=[CS256], sem=CS256.lsem)
        S.dma("sp", IDB.ap, identb_d, writes=[IDB], sem=IDB.lsem)

        def layer_setup(l):
            kw = dict(allow_slow_non_contiguous=True)
            S.dma("sp", IDF.ap, identf_d, writes=[IDF], sem=IDF.lsem)
            S.dma("sp", ONES.ap[0:1, :], ones_d, writes=[ONES], sem=ONES.lsem)
            S.dma("sp", GT.ap, norm_g[l].rearrange("(dc p) -> p dc", p=128), writes=[GT], sem=GT.lsem, **kw)
            S.dma("sp", GCOL.ap, ln_g[l].rearrange("(h c) -> c h", c=128), writes=[GCOL], sem=GCOL.lsem, **kw)
            S.dma("sp", BGA.ap, b_gate[l, 0].rearrange("(e p) -> p e", p=128), writes=[BGA], sem=BGA.lsem, **kw)
            S.dma("sp", BGB.ap, b_gate[l, 1].rearrange("(e p) -> p e", p=128), writes=[BGB], sem=BGB.lsem, **kw)
            WSN = XIN[0]
            wsn = view(6 * PAGE, 8192, F32).rearrange("p (h q) -> p h q", q=128)
            wst32 = view(6 * PAGE + 8192, 8192, F32).rearrange("p (h q) -> p h q", q=128)
            TB = XIN[1]
            brep = view(7 * PAGE, 8192, F32)
            bsrow = view(7 * PAGE + 8192, 8192, F32)
            S.dma("sp", wsn, w_sp[l].rearrange("h p q -> p h q"), writes=[WSN], sem=WSN.lsem)
            S.dma("sp", brep, ln_b[l].partition_broadcast(128), writes=[TB], sem=TB.lsem)
            S.dma("sp", bsrow[0:1, :], b_sp[l:l + 1, :], writes=[TB], sem=TB.lsem)
            for hg in range(4):
                bank = next_bank()

                def fn(e, hg=hg, bank=bank):
                    ins = None
                    for i in range(4):
                        h = hg * 4 + i
                        ins = e.transpose(out=bank.ap[:, i * 128:(i + 1) * 128], in_=wsn[:, h, :], identity=IDF.ap)
                    return ins

                S.op("pe", fn, reads=[WSN, IDF], writes=[bank])
                o32 = wst32[:, hg * 4:(hg + 1) * 4, :]
                ob = WSTB.ap[:, hg * 4:(hg + 1) * 4, :]
                src = bank.ap.rearrange("p (a b) -> p a b", b=128)
                S.op("dve", lambda e, o32=o32, src=src: e.tensor_copy(out=o32, in_=src), reads=[bank], writes=[WSN])
                S.op("dve", lambda e, ob=ob, src=src: e.tensor_copy(out=ob, in_=src), reads=[bank], writes=[WSTB])
            for hg in range(4):
                bank = next_bank()
                mms = []
                for i in range(4):
                    h = hg * 4 + i
                    o = bank.ap[:, i * 128:(i + 1) * 128]
                    mms.append((o, brep[:, h * 128:(h + 1) * 128], wst32[:, h, :], True, False))
                    mms.append((o, ONES.ap[0:1, :], bsrow[0:1, h * 128:(h + 1) * 128], False, True))
                mm_unit(bank, mms, reads=[WSN, TB, ONES])
                ob = B2.ap[:, hg * 4:(hg + 1) * 4, :]
                src = bank.ap.rearrange("p (a b) -> p a b", b=128)
                S.op("dve", lambda e, ob=ob, src=src: e.tensor_copy(out=ob, in_=src), reads=[bank], writes=[B2])

        Q = {}

        def mkq(l, name, nslot, elems):
            q = Buf(f"q_{name}{l}", [Region(f"q_{name}{l}")])
            q.ap = dscr(f"q_{name}{l}", [nslot, 128, elems], BF16)
            q.sem = sem_alloc(f"q_{name}{l}")
            Q[(l, name)] = q
            return q

        for l in range(DEPTH):
            mkq(l, "xb", 16, 4096)
            mkq(l, "v", 8, 8192)
            mkq(l, "zb", 16, 4096)
            mkq(l, "u", 16, 4096)
            mkq(l, "za", 16, 4096)
            mkq(l, "ga", 32, 4096)
            mkq(l, "gb", 32, 4096)
            mkq(l, "wab", 32, 4096)
            mkq(l, "wo", 16, 8192)

        gates = {}

        for nm, ns_ in (("xb", 16), ("v", 8)):
            q0 = Q[(0, nm)]
            q0.slots = []
            for i in range(ns_):
                sb = Buf(f"q_{nm}0_{i}", [Region(f"q_{nm}0_{i}")])
                sb.ap = q0.ap
                sb.sem = sem_alloc(f"q_{nm}0_{i}")
                q0.slots.append(sb)

        def conv(l, name, idx, src, ncols, sub=None, gate=None):
            q = Q[(l, name)]
            if hasattr(q, "slots"):
                q = q.slots[idx]
            dst = q.ap[idx].rearrange("p (a b) -> p a b", b=ncols)
            if sub is not None:
                dst = dst[:, sub[0]:sub[1], :]
            S.dma("pool", dst, src, reads=([gate] if gate is not None else []), writes=[q], sem=q.sem)

        def convert_layer(l, gate_fn):
            wl = w_in[l]
            for j in range(16):
                conv(l, "xb", j, wcols(wl, 6144 + j * 128, 128), 128, gate=gate_fn(0))
            for eg in range(8):
                conv(l, "v", eg, wcols(wl, 2048 + eg * 256, 256), 256, gate=gate_fn(0))
            for j in range(16):
                conv(l, "zb", j, wcols(wl, 8192 + j * 128, 128), 128, gate=gate_fn(1))
            for h in range(16):
                conv(l, "u", h, wcols(wl, h * 128, 128), 128, gate=gate_fn(2))
                conv(l, "za", h, wcols(wl, 4096 + h * 128, 128), 128, gate=gate_fn(2))
            for e in range(32):
                g = gate_fn(3 + e // 8)
                conv(l, "ga", e, wcols(wl, 10240 + e * 128, 128), 128, gate=g)
                conv(l, "wab", e, wcols(w_a[l], e * 128, 128), 128, sub=(0, 16), gate=g)
                conv(l, "wab", e, wcols(w_b[l], e * 128, 128), 128, sub=(16, 32), gate=g)
                conv(l, "gb", e, wcols(wl, 14336 + e * 128, 128), 128, gate=g)
            for eg in range(16):
                conv(l, "wo", eg, wcols(w_out[l], eg * 256, 256), 256, gate=gate_fn(7))

        def pass_a(l, xsrc, xsrc_r):
            groups = []
            XSB = [XS, ABT]

            def xload(t, s):
                xin = XIN[s % 2]
                r0 = t * 512 + s * 128
                S.dma("sp", xin.ap, xsrc[r0:r0 + 128, :], reads=[xsrc_r], writes=[xin], sem=xin.lsem)

            for t in range(NT):
                def n_phase(_, t=t):
                    if t == 0:
                        xload(0, 0)
                        xload(0, 1)
                    for s in range(4):
                        xin = XIN[s % 2]
                        xs = XSB[s % 2]
                        sn = SN[s % 2]
                        ssq = sn.ap[:, 0:1]
                        rs = sn.ap[:, 1:2]
                        S.op("act", lambda e, xin=xin, xs=xs, ssq=ssq: e.activation(out=xs.ap, in_=xin.ap, func=AF.Square, accum_out=ssq),
                             reads=[xin], writes=[xs, sn])
                        S.op("act", lambda e, rs=rs, ssq=ssq: e.activation(out=rs, in_=ssq, func=AF.Sqrt, scale=1.0 / D, bias=EPS),
                             reads=[sn], writes=[sn])
                        S.op("dve", lambda e, rs=rs: e.reciprocal(out=rs, in_=rs), reads=[sn], writes=[sn])
                        S.op("act", lambda e, xin=xin, xs=xs, rs=rs: e.activation(out=xs.ap[:, 0:2048], in_=xin.ap[:, 0:2048], func=AF.Copy, scale=rs),
                             reads=[xin, sn], writes=[xs])
                        S.op("dve", lambda e, xin=xin, xs=xs, rs=rs: e.tensor_scalar(out=xs.ap[:, 2048:4096], in0=xin.ap[:, 2048:4096], scalar1=rs, scalar2=None, op0=ALU.mult),
                             reads=[xin, sn], writes=[xs])
                        if s + 2 < 4:
                            xload(t, s + 2)
                        for dg in range(4):
                            bank = next_bank()

                            def fn(e, dg=dg, bank=bank, xs=xs):
                                ins = None
                                for i in range(8):
                                    dc = dg * 8 + i
                                    ins = e.transpose(out=bank.apb[:, i * 128:(i + 1) * 128],
                                                      in_=xs.ap[:, dc * 128:(dc + 1) * 128], identity=IDB.ap)
                                return ins

                            S.op("pe", fn, reads=[xs, IDB], writes=[bank])
                            o = HT.ap[:, dg * 8:(dg + 1) * 8, s * 128:(s + 1) * 128]
                            src = bank.apb.rearrange("p (a b) -> p a b", b=128)
                            g = GT.ap[:, dg * 8:(dg + 1) * 8].unsqueeze(2).to_broadcast([128, 8, 128])
                            S.op("dve", lambda e, o=o, src=src, g=g: e.tensor_tensor(out=o, in0=src, in1=g, op=ALU.mult),
                                 reads=[bank, GT], writes=[HT])
                    S.dma("sp", hT_s[t], HT.ap, reads=[HT], writes=[hT_r], sem=HT.ssem)

                groups.append((lambda: None, n_phase, 2))
                for j in range(16):
                    def ld(j=j):
                        return load_q(Q[(l, "xb")], j, 128)

                    def cp(sl, j=j):
                        bank = next_bank()
                        mms = [(bank.ap, sl.ap[:, dc, :], HT.ap[:, dc, :], dc == 0, dc == 31) for dc in range(32)]
                        mm_unit(bank, mms, reads=[sl, HT])
                        evac_copy(XBT.ap[:, j, :], bank.ap, reads=[bank], writes=[XBT])

                    groups.append((ld, cp, 2))
                def cd_phase(_, t=t):
                    for s in range(4):
                        abt = XSB[(s + 1) % 2]
                        for g in range(8):
                            bank = next_bank()
                            mms = [(bank.ap, XBT.ap[:, 2 * g + kc, s * 128:(s + 1) * 128], CS256.ap[:, kc, :], kc == 0, kc == 1) for kc in range(2)]
                            mm_unit(bank, mms, reads=[XBT, CS256])
                            evac_copy(abt.ap[:, g * 512:(g + 1) * 512], bank.ap, reads=[bank], writes=[abt])
                        ncx = t * 4 + s
                        S.dma("sp", AB_s[:, :, ncx, :].rearrange("j p e -> p j e"), abt.ap.rearrange("p (j e) -> p j e", e=256),
                              reads=[abt], writes=[AB_r], sem=abt.ssem)

                groups.append((lambda: None, cd_phase, 2))
                for eg in range(8):
                    def ld(eg=eg):
                        return load_q(Q[(l, "v")], eg, 256, double=True)

                    def cp(sl, t=t, eg=eg):
                        for s in range(4):
                            bank = next_bank()
                            o = bank.ap[:, 0:256]
                            mms = [(o, HT.ap[:, dc, s * 128:(s + 1) * 128], sl.ap[:, dc, :], dc == 0, dc == 31) for dc in range(32)]
                            mm_unit(bank, mms, reads=[sl, HT])
                            dst = GV.ap[:, s, eg * 256:(eg + 1) * 256]
                            S.op("act", lambda e, dst=dst, o=o: e.activation(out=dst, in_=o, func=AF.Gelu_apprx_tanh),
                                 reads=[bank], writes=[GV])
                        if eg == 6 and t + 1 < NT:
                            xload(t + 1, 0)
                            xload(t + 1, 1)

                    groups.append((ld, cp, 2))

                def ln_phase(_, t=t):
                    def cols(s):
                        c0 = 8 + 28 * (s % 2)
                        return stat(c0, c0 + 24), stat(c0 + 24, c0 + 26), stat(c0 + 26, c0 + 27)

                    def stage1(s):
                        sl_ = SLNS[s % 2]
                        bstf, mv, rs2 = cols(s)
                        bst = bstf.rearrange("p (a b) -> p a b", b=6)
                        for q in range(4):
                            S.op("dve", lambda e, s=s, q=q, bst=bst: e.bn_stats(out=bst[:, q, :], in_=GV.ap[:, s, q * 512:(q + 1) * 512]),
                                 reads=[GV], writes=[sl_])
                        S.op("dve", lambda e, mv=mv, bstf=bstf: e.bn_aggr(out=mv, in_=bstf), reads=[sl_], writes=[sl_])
                        S.op("act", lambda e, mv=mv, rs2=rs2: e.activation(out=rs2, in_=mv[:, 1:2], func=AF.Sqrt, bias=EPS), reads=[sl_], writes=[sl_])

                    def stage2(s):
                        sl_ = SLNS[s % 2]
                        bstf, mv, rs2 = cols(s)
                        S.op("dve", lambda e, rs2=rs2: e.reciprocal(out=rs2, in_=rs2), reads=[sl_], writes=[sl_])
                        S.op("dve", lambda e, s=s, mv=mv, rs2=rs2: e.tensor_scalar(out=YB.ap[:, s, :], in0=GV.ap[:, s, :], scalar1=mv[:, 0:1], scalar2=rs2,
                                                                 op0=ALU.subtract, op1=ALU.mult),
                             reads=[GV, sl_], writes=[YB])

                    stage1(0)
                    stage1(1)
                    stage2(0)
                    stage1(2)
                    stage2(1)
                    stage1(3)
                    stage2(2)
                    stage2(3)
                    S.dma("sp", y_s[t * 512:(t + 1) * 512, :].rearrange("(s p) c -> p s c", p=128), YB.ap,
                          reads=[YB], writes=[y_r], sem=YB.ssem)

                groups.append((lambda: None, ln_phase, 2))
            run_groups(groups)

        def pass_b(l, xsrc, xsrc_r, xdst, xdst_r):
            groups = []

            def load_ht(t):
                S.dma("sp", HT.ap, hT_s[t], reads=[hT_r], writes=[HT], sem=HT.lsem)
                g = Buf("gate", [Region("gate")])
                g.regions[0].writers = {HT.lsem: HT.lsem.count}
                gates[(l, t)] = g

            for t in range(NT):
                for half in range(2):
                    for j in range(16):
                        def ld(j=j, half=half):
                            z = load_q(Q[(l, "zb")], j, 128) if half == 0 else None
                            ab = next_slot(double=True)
                            S.dma("sp", ab.flat[:, 0:NCH * 256], AB_s[j].rearrange("p a b -> p (a b)"), reads=[AB_r], writes=[ab], sem=ab.lsem)
                            ab.ap = ab.flat.rearrange("p (a b) -> p a b", b=256)
                            return (z, ab)

                        def cp(sl, t=t, half=half, j=j):
                            zsl, absl = sl
                            if j == 0:
                                if half == 0 and t == 0:
                                    load_ht(0)
                                if half == 0:
                                    S.dma("sp", SLP["p7"].ap, dft_d[t * 2][:, 0:NCL], reads=[], writes=[SLP["p7"]], sem=SLP["p7"].lsem)
                                    S.dma("sp", SLP["p8"].ap, dft_d[t * 2][:, NCL:NCH], reads=[], writes=[SLP["p8"]], sem=SLP["p8"].lsem)
                                    S.dma("sp", SLP["p10"].ap, dft_d[t * 2 + 1][:, 0:NCL], reads=[], writes=[SLP["p10"]], sem=SLP["p10"].lsem)
                                else:
                                    S.dma("sp", SLP["p7"].ap, dft_d[t * 2 + 1][:, NCL:NCH], reads=[], writes=[SLP["p7"]], sem=SLP["p7"].lsem)
                            slo, shi = (SLP["p7"], SLP["p8"]) if half == 0 else (SLP["p10"], SLP["p7"])
                            if half == 0:
                                bz = next_bank()
                                mms = [(bz.ap, zsl.ap[:, dc, :], HT.ap[:, dc, :], dc == 0, dc == 31) for dc in range(32)]
                                mm_unit(bz, mms, reads=[zsl, HT])
                            bf = next_bank()
                            of = bf.ap[:, 0:256]
                            mms = []
                            for n in range(NCL):
                                mms.append((of, absl.ap[:, n, 0:128], slo.ap[:, n, 0, :], n == 0, False))
                                mms.append((of, absl.ap[:, n, 128:256], slo.ap[:, n, 1, :], False, False))
                            mm_unit(bf, mms, reads=[absl, slo])
                            mms = []
                            for n in range(NCL):
                                mms.append((of, absl.ap[:, NCL + n, 0:128], shi.ap[:, n, 0, :], False, False))
                                mms.append((of, absl.ap[:, NCL + n, 128:256], shi.ap[:, n, 1, :], False, n == NCL - 1))
                            mm_unit(bf, mms, reads=[absl, shi], cont=True)
                            dst = FB.ap[:, j, half * 256:(half + 1) * 256]
                            if half == 0:
                                tm = TMP[j % 2]
                                S.op("act", lambda e, tm=tm, bz=bz: e.activation(out=tm.ap[:, 0:256], in_=bz.ap[:, 0:256], func=AF.Silu),
                                     reads=[bz], writes=[tm])
                                S.op("act", lambda e, bz=bz: e.activation(out=SZ2.ap[:, j, :], in_=bz.ap[:, 256:512], func=AF.Silu),
                                     reads=[bz], writes=[SZ2])
                                S.op("dve", lambda e, dst=dst, of=of, tm=tm: e.tensor_tensor(out=dst, in0=of, in1=tm.ap[:, 0:256], op=ALU.mult),
                                     reads=[bf, tm], writes=[FB])
                            else:
                                S.op("dve", lambda e, dst=dst, of=of: e.tensor_tensor(out=dst, in0=of, in1=SZ2.ap[:, j, :], op=ALU.mult),
                                     reads=[bf, SZ2], writes=[FB])
                                if j == 15:
                                    S.dma("sp", YT.ap, y_s[t * 512:(t + 1) * 512, :].rearrange("(s p) c -> p s c", p=128),
                                          reads=[y_r], writes=[YT], sem=YT.lsem)

                        groups.append((ld, cp, 1))
                for h in range(16):
                    def ld(h=h):
                        return (load_q(Q[(l, "u")], h, 128), load_q(Q[(l, "za")], h, 128))

                    def cp(sl, t=t, h=h):
                        usl, zsl = sl
                        bu = next_bank()
                        mm_unit(bu, [(bu.ap, usl.ap[:, dc, :], HT.ap[:, dc, :], dc == 0, dc == 31) for dc in range(32)], reads=[usl, HT])
                        bz = next_bank()
                        mm_unit(bz, [(bz.ap, zsl.ap[:, dc, :], HT.ap[:, dc, :], dc == 0, dc == 31) for dc in range(32)], reads=[zsl, HT])
                        bs = next_bank()
                        mms = [(bs.ap[:, c * 128:(c + 1) * 128], YT.ap[:, c, h * 128:(h + 1) * 128], WSTB.ap[:, h, :], True, True) for c in range(4)]
                        mm_unit(bs, mms, reads=[YT, WSTB])
                        k = (h % 2) * 4
                        t1, t2, t3 = TMP[k], TMP[k + 1], TMP[k + 2]
                        S.op("act", lambda e, t1=t1, bu=bu: e.activation(out=t1.ap, in_=bu.ap, func=AF.Gelu_apprx_tanh), reads=[bu], writes=[t1])
                        S.op("act", lambda e, t2=t2, bz=bz: e.activation(out=t2.ap, in_=bz.ap, func=AF.Silu), reads=[bz], writes=[t2])
                        S.op("dve", lambda e, t3=t3, bs=bs, h=h: e.scalar_tensor_tensor(
                            out=t3.ap.rearrange("p (a b) -> p a b", b=128), in0=bs.ap.rearrange("p (a b) -> p a b", b=128),
                            scalar=GCOL.ap[:, h:h + 1], in1=B2.ap[:, h:h + 1, :].to_broadcast([128, 4, 128]),
                            op0=ALU.mult, op1=ALU.add), reads=[bs, GCOL, B2], writes=[t3])
                        S.op("dve", lambda e, t1=t1, t2=t2: e.tensor_tensor(out=t1.ap, in0=t1.ap, in1=t2.ap, op=ALU.mult),
                             reads=[t1, t2], writes=[t1])
                        S.op("dve", lambda e, t1=t1, t3=t3, h=h: e.tensor_tensor(out=AT.ap[:, h, :], in0=t3.ap, in1=t1.ap, op=ALU.mult),
                             reads=[t1, t3], writes=[AT])

                    groups.append((ld, cp, 2))
                for ech in range(32):
                    def ld(ech=ech):
                        return (load_q(Q[(l, "ga")], ech, 128), load_q(Q[(l, "wab")], ech, 128), load_q(Q[(l, "gb")], ech, 128))

                    def cp(sl, ech=ech):
                        ga, wab, gb = sl
                        b_ga = next_bank()
                        mm_unit(b_ga, [(b_ga.ap, ga.ap[:, dc, :], HT.ap[:, dc, :], dc == 0, dc == 31) for dc in range(32)], reads=[ga, HT])
                        b_ya = next_bank()
                        mm_unit(b_ya, [(b_ya.ap, wab.ap[:, kc, :], AT.ap[:, kc, :], kc == 0, kc == 15) for kc in range(16)], reads=[wab, AT])
                        b_gb = next_bank()
                        mm_unit(b_gb, [(b_gb.ap, gb.ap[:, dc, :], HT.ap[:, dc, :], dc == 0, dc == 31) for dc in range(32)], reads=[gb, HT])
                        b_yb = next_bank()
                        mm_unit(b_yb, [(b_yb.ap, wab.ap[:, 16 + kc, :], FB.ap[:, kc, :], kc == 0, kc == 15) for kc in range(16)], reads=[wab, FB])
                        k = (ech % 2) * 4
                        ta, tb = TMP[k], TMP[k + 1]
                        S.op("act", lambda e, ta=ta, b_ga=b_ga: e.activation(out=ta.ap, in_=b_ga.ap, func=AF.Sigmoid, bias=BGA.ap[:, ech:ech + 1]),
                             reads=[b_ga, BGA], writes=[ta])
                        S.op("act", lambda e, tb=tb, b_gb=b_gb: e.activation(out=tb.ap, in_=b_gb.ap, func=AF.Sigmoid, bias=BGB.ap[:, ech:ech + 1]),
                             reads=[b_gb, BGB], writes=[tb])
                        S.op("dve", lambda e, ta=ta, b_ya=b_ya: e.tensor_tensor(out=ta.ap, in0=b_ya.ap, in1=ta.ap, op=ALU.mult),
                             reads=[b_ya, ta], writes=[ta])
                        S.op("dve", lambda e, tb=tb, b_yb=b_yb: e.tensor_tensor(out=tb.ap, in0=b_yb.ap, in1=tb.ap, op=ALU.mult),
                             reads=[b_yb, tb], writes=[tb])
                        S.op("dve", lambda e, ta=ta, tb=tb: e.tensor_tensor(out=MT.ap[:, ech, :], in0=ta.ap, in1=tb.ap, op=ALU.add),
                             reads=[ta, tb], writes=[MT])

                    groups.append((ld, cp, 1))
                for eg in range(16):
                    def ld(eg=eg):
                        return load_q(Q[(l, "wo")], eg, 256, double=True)

                    def cp(wo, t=t, eg=eg):
                        xr, ot = XR[eg % 2], OT[eg % 2]
                        if eg == 1 and t + 1 < NT:
                            load_ht(t + 1)
                        S.dma("sp", xr.ap, xsrc[t * 512:(t + 1) * 512, eg * 256:(eg + 1) * 256].rearrange("(s p) e -> p s e", p=128),
                              reads=[xsrc_r], writes=[xr], sem=xr.lsem)
                        for s in range(4):
                            bank = next_bank()
                            o = bank.ap[:, 0:256]
                            mm_unit(bank, [(o, MT.ap[:, dc, s * 128:(s + 1) * 128], wo.ap[:, dc, :], dc == 0, dc == 31) for dc in range(32)],
                                    reads=[wo, MT])
                            S.op("dve", lambda e, ot=ot, xr=xr, o=o, s=s: e.tensor_tensor(out=ot.ap[:, s, :], in0=o, in1=xr.ap[:, s, :], op=ALU.add),
                                 reads=[bank, xr], writes=[ot])
                        S.dma("sp", xdst[t * 512:(t + 1) * 512, eg * 256:(eg + 1) * 256].rearrange("(s p) e -> p s e", p=128), ot.ap,
                              reads=[ot], writes=[xdst_r], sem=ot.ssem)

                    groups.append((ld, cp, 2))
            run_groups(groups)

        def pass_c(xsrc, xsrc_r):
            FG = HT
            fg = view(4 * PAGE, PAGE, F32)
            S.dma("sp", fg, final_g.partition_broadcast(128), reads=[], writes=[FG], sem=FG.lsem)
            XC = []
            for i in range(6):
                bb = mkbuf(f"xc{i}", [pages[6 + i]] if i < 5 else [TMP[k].regions[0] for k in range(8)], load=True, store=True)
                bb.ap = view((6 + i) * PAGE, PAGE, F32)
                XC.append(bb)
            JK = Buf("junkc", slot_regs[0:1])
            jk = view(0, 8192, BF16)
            for r in range(TOK // 128):
                xin = XC[r % 6]
                sn = SN[r % 2]
                ssq = sn.ap[:, 0:1]
                rs = sn.ap[:, 1:2]
                S.dma("sp", xin.ap, xsrc[r * 128:(r + 1) * 128, :], reads=[xsrc_r], writes=[xin], sem=xin.lsem)
                S.op("act", lambda e, xin=xin, ssq=ssq: e.activation(out=jk, in_=xin.ap, func=AF.Square, accum_out=ssq),
                     reads=[xin], writes=[JK, sn])
                S.op("act", lambda e, rs=rs, ssq=ssq: e.activation(out=rs, in_=ssq, func=AF.Sqrt, scale=1.0 / D, bias=EPS), reads=[sn], writes=[sn])
                S.op("dve", lambda e, rs=rs: e.reciprocal(out=rs, in_=rs), reads=[sn], writes=[sn])
                S.op("dve", lambda e, xin=xin, rs=rs: e.scalar_tensor_tensor(out=xin.ap, in0=xin.ap, scalar=rs, in1=fg, op0=ALU.mult, op1=ALU.mult),
                     reads=[xin, sn, FG], writes=[xin])
                S.dma("sp", y_out[r * 128:(r + 1) * 128, :], xin.ap, reads=[xin], writes=[yout_r], sem=xin.ssem)

        xsrc, xsrc_r = x_in, dram_buf("x_in")
        dsts = [(xa_s, xa_r), (xb_s, xb_r)]
        convert_layer(0, lambda k: None)
        for l in range(DEPTH):
            layer_setup(l)
            pass_a(l, xsrc, xsrc_r)
            xdst, xdst_r = dsts[l % 2]
            pass_b(l, xsrc, xsrc_r, xdst, xdst_r)
            if l + 1 < DEPTH:
                convert_layer(l + 1, lambda k, l=l: gates[(l, min(k, NT - 1))])
            xsrc, xsrc_r = xdst, xdst_r
        pass_c(xsrc, xsrc_r)
        S.final_wait("sp", [yout_r])

        with nc.Block() as block:
            @block.tensor
            def _(e):
                Sched.run(S.engs["pe"], e)

            @block.scalar
            def _(e):
                Sched.run(S.engs["act"], e)

            @block.vector
            def _(e):
                Sched.run(S.engs["dve"], e)

            @block.gpsimd
            def _(e):
                Sched.run(S.engs["pool"], e)

            @block.sync
            def _(e):
                Sched.run(S.engs["sp"], e)
    return nc


def make_cs256():
    c = np.arange(256)[:, None]
    j = np.arange(256)[None, :]
    ang = 2.0 * np.pi * ((c * j) % 256) / 256.0
    C = np.cos(ang) / 16.0
    Sn = np.sin(ang) / 16.0
    full = np.concatenate([C[:, 0:128], Sn[:, 0:128], C[:, 128:256], Sn[:, 128:256]], axis=1)
    out = full.reshape(2, 128, 512).transpose(1, 0, 2)
    return np.ascontiguousarray(out).astype(ml_dtypes.bfloat16)


def make_dft(TOK, SEQ):
    NCH = TOK // 128
    NHALF = TOK // 256
    n = np.arange(TOK)
    out = np.zeros((NHALF, 128, NCH, 2, 256), dtype=ml_dtypes.bfloat16)
    scale = 1.0 / np.sqrt(SEQ)
    for hh in range(NHALF):
        k = hh * 256 + np.arange(256)
        same = (n[:, None] // SEQ) == (k[None, :] // SEQ)
        m = ((n[:, None] % SEQ) * (k[None, :] % SEQ)) % SEQ
        ang = 2.0 * np.pi * m / SEQ
        Cm = np.where(same, np.cos(ang), 0.0) * scale
        Sm = np.where(same, -np.sin(ang), 0.0) * scale
        out[hh, :, :, 0, :] = Cm.reshape(NCH, 128, 256).transpose(1, 0, 2).astype(ml_dtypes.bfloat16)
        out[hh, :, :, 1, :] = Sm.reshape(NCH, 128, 256).transpose(1, 0, 2).astype(ml_dtypes.bfloat16)
    return out


_CACHE = {}


def run_cores(x_list, seqs, weights, DEPTH):
    TOK = x_list[0].shape[0]
    key = (TOK, DEPTH)
    if key not in _CACHE:
        _CACHE[key] = build_program(TOK, DEPTH)
    nc = _CACHE[key]
    cs256 = make_cs256()
    dfts = {}
    for sq in set(seqs):
        dfts[sq] = make_dft(TOK, sq)
    identb = np.eye(128, dtype=np.float32).astype(ml_dtypes.bfloat16)
    identf = np.eye(128, dtype=np.float32)
    ones = np.ones((1, 128), dtype=np.float32)
    in_maps = []
    for xc, sq in zip(x_list, seqs):
        m = dict(weights)
        m["x"] = np.ascontiguousarray(xc)
        m["cs256"] = cs256
        m["dft"] = dfts[sq]
        m["identb"] = identb
        m["identf"] = identf
        m["ones"] = ones
        in_maps.append(m)
    res = run_bass_kernel_spmd(nc, in_maps, core_ids=list(range(len(x_list))))
    return [np.asarray(r["y"]) for r in res.results]


def kernel(x_prompt, x_sample, norm_g, w_in, sgu_ln_g, sgu_ln_b, w_spatial, b_spatial, w_a, w_b, b_gate, w_out, final_g):
    f = lambda a: np.ascontiguousarray(np.asarray(a, dtype=np.float32))
    x_prompt, x_sample = f(x_prompt), f(x_sample)
    DEPTH = int(np.asarray(norm_g).shape[0])
    weights = {
        "w_in": f(w_in), "w_a": f(w_a), "w_b": f(w_b), "w_out": f(w_out), "norm_g": f(norm_g),
        "sgu_ln_g": f(sgu_ln_g), "sgu_ln_b": f(sgu_ln_b), "w_spatial": f(w_spatial),
        "b_spatial": f(b_spatial).reshape(DEPTH, 16 * 128), "b_gate": f(b_gate), "final_g": f(final_g),
    }
    B, SQ, _ = x_prompt.shape
    B2_, SQ2, _ = x_sample.shape
    x_list, seqs = [], []
    for c in range(4):
        x_list.append(x_prompt[2 * c:2 * c + 2].reshape(2 * SQ, D))
        seqs.append(SQ)
    for c in range(4):
        x_list.append(x_sample[c].reshape(SQ2, D))
        seqs.append(SQ2)
    outs = run_cores(x_list, seqs, weights, DEPTH)
    y_prompt = np.stack([outs[c].reshape(2, SQ, D) for c in range(4)], axis=0).reshape(B, SQ, D)
    y_sample = np.stack([outs[4 + c].reshape(SQ2, D) for c in range(4)], axis=0)
    return (y_prompt.astype(np.float32), y_sample.astype(np.float32))
```

```python
import numpy as np
import ml_dtypes
import concourse.bass as bass
import concourse.mybir as mybir
from concourse.bass_utils import run_bass_kernel_spmd

F32 = mybir.dt.float32
BF16 = mybir.dt.bfloat16
AF = mybir.ActivationFunctionType
ALU = mybir.AluOpType

D = 4096
D_A = 2048
D_IN = 18432
EPS = 1e-6
NS = 8


class Sem:
    def __init__(self, h, name):
        self.h = h
        self.name = name
        self.count = 0


class Region:
    def __init__(self, name):
        self.name = name
        self.writers = {}
        self.readers = {}


class Buf:
    def __init__(self, name, regions):
        self.name = name
        self.regions = regions
        self.lsem = None
        self.ssem = None


class Eng:
    def __init__(self, name, sem):
        self.name = name
        self.sem = sem
        self.seen = {}
        self.prog = []


class Sched:
    def __init__(self, nc, sem_alloc):
        self.nc = nc
        self.sem_alloc = sem_alloc
        self.engs = {}
        for n in ("pe", "act", "dve", "pool", "sp"):
            self.engs[n] = Eng(n, sem_alloc("e_" + n))

    def _deps(self, reads, writes):
        deps = {}

        def add(d):
            for s, v in d.items():
                if deps.get(s, 0) < v:
                    deps[s] = v

        for b in reads:
            for r in b.regions:
                add(r.writers)
        for b in writes:
            for r in b.regions:
                add(r.writers)
                add(r.readers)
        return deps

    def _commit(self, reads, writes, sem, val):
        wr = set()
        for b in writes:
            for r in b.regions:
                wr.add(id(r))
                if r.readers:
                    r.writers = {}
                    r.readers = {}
                r.writers[sem] = max(r.writers.get(sem, 0), val)
        for b in reads:
            for r in b.regions:
                if id(r) in wr:
                    continue
                r.readers[sem] = max(r.readers.get(sem, 0), val)

    def _waits(self, eng, deps):
        waits = []
        for s, v in deps.items():
            if eng.seen.get(s, 0) < v:
                eng.seen[s] = v
                waits.append((s, v))
        return waits

    def op(self, engname, fn, reads=(), writes=(), skip_own=False):
        eng = self.engs[engname]
        deps = self._deps(reads, writes)
        if skip_own:
            deps.pop(eng.sem, None)
        waits = self._waits(eng, deps)
        eng.sem.count += 1
        val = eng.sem.count
        eng.seen[eng.sem] = max(eng.seen.get(eng.sem, 0), 0)
        eng.prog.append((waits, fn, eng.sem, 1))
        self._commit(reads, writes, eng.sem, val)

    def dma(self, qname, out, in_, reads=(), writes=(), sem=None, **kw):
        eng = self.engs[qname]
        deps = self._deps(reads, writes)
        waits = self._waits(eng, deps)
        sem.count += 16
        val = sem.count

        def fn(e, out=out, in_=in_, kw=kw):
            return e.dma_start(out=out, in_=in_, **kw)

        eng.prog.append((waits, fn, sem, 16))
        self._commit(reads, writes, sem, val)

    def final_wait(self, qname, bufs):
        eng = self.engs[qname]
        deps = self._deps((), bufs)
        waits = self._waits(eng, deps)
        eng.prog.append((waits, None, None, 0))

    @staticmethod
    def run(eng, e):
        for waits, fn, sem, amt in eng.prog:
            for s, v in waits:
                e.wait_ge(s.h, v)
            if fn is not None:
                ins = fn(e)
                ins.then_inc(sem.h, amt)


def build_program(TOK, DEPTH):
    NT = TOK // 512
    NCH = TOK // 128
    NHALF = TOK // 256
    nc = bass.Bass("TRN2", target_bir_lowering=False)

    def din(name, shape, dt=F32):
        return nc.dram_tensor(name, list(shape), dt, kind="ExternalInput").ap()

    def dscr(name, shape, dt):
        return nc.dram_tensor(name, list(shape), dt, kind="Internal").ap()

    x_in = din("x", [TOK, D])
    w_in = din("w_in", [DEPTH, D, D_IN])
    w_a = din("w_a", [DEPTH, D_A, D])
    w_b = din("w_b", [DEPTH, D_A, D])
    w_out = din("w_out", [DEPTH, D, D])
    norm_g = din("norm_g", [DEPTH, D])
    ln_g = din("sgu_ln_g", [DEPTH, D_A])
    ln_b = din("sgu_ln_b", [DEPTH, D_A])
    w_sp = din("w_spatial", [DEPTH, 16, 128, 128])
    b_sp = din("b_spatial", [DEPTH, 16 * 128])
    b_gate = din("b_gate", [DEPTH, 2, D])
    final_g = din("final_g", [D])
    cs256_d = din("cs256", [128, 2, 512], BF16)
    dft_d = din("dft", [NHALF, 128, NCH, 2, 256], BF16)
    identb_d = din("identb", [128, 128], BF16)
    identf_d = din("identf", [128, 128], F32)
    ones_d = din("ones", [1, 128], F32)
    y_out = nc.dram_tensor("y", [TOK, D], F32, kind="ExternalOutput").ap()

    hT_s = dscr("hT_s", [NT, 128, 32, 512], BF16)
    AB_s = dscr("AB_s", [16, 128, NCH, 256], BF16)
    y_s = dscr("y_s", [TOK, D_A], BF16)
    xa_s = dscr("xa_s", [TOK, D], F32)
    xb_s = dscr("xb_s", [TOK, D], F32)

    import contextlib

    with contextlib.ExitStack() as st:
        PAGE = 16384
        NPAGE = 12
        TAIL = 16128
        big = st.enter_context(nc.sbuf_tensor("big", [128, (NPAGE * PAGE + TAIL) // 2], BF16))
        banks = [st.enter_context(nc.psum_tensor(f"ps{i}", [128, 512], F32)) for i in range(8)]
        sem_list = []

        def sem_alloc(name):
            h = st.enter_context(nc.semaphore(name))
            s = Sem(h, name)
            sem_list.append(s)
            return s

        S = Sched(nc, sem_alloc)

        def view(off, nbytes, dt):
            v = big[:, off // 2:(off + nbytes) // 2]
            if dt == F32:
                v = v.bitcast(F32)
            return v

        pages = [Region(f"page{i}") for i in range(NPAGE)]

        def mkbuf(name, regions, load=False, store=False):
            b = Buf(name, regions)
            if load:
                b.lsem = sem_alloc("l_" + name)
            if store:
                b.ssem = sem_alloc("s_" + name)
            return b

        SLOT = 8192
        slot_regs = [Region(f"slot{i}") for i in range(NS)]
        slot_sems = [sem_alloc(f"l_slot{i}") for i in range(NS)]
        slot_ctr = [0]

        def next_slot(double=False):
            if double and slot_ctr[0] % 2 == 1:
                slot_ctr[0] += 1
            i = slot_ctr[0] % NS
            n = 2 if double else 1
            slot_ctr[0] += n
            b = Buf(f"slot{i}x{n}", slot_regs[i:i + n])
            b.lsem = slot_sems[i]
            b.flat = view(i * SLOT, n * SLOT, BF16)
            return b

        HT = mkbuf("HT", [pages[4], pages[5]], load=True, store=True)
        HT.ap = view(4 * PAGE, 2 * PAGE, BF16).rearrange("p (a b) -> p a b", b=512)

        XIN = []
        for i in range(2):
            b = mkbuf(f"xin{i}", [pages[6 + i]], load=True, store=True)
            b.ap = view((6 + i) * PAGE, PAGE, F32)
            XIN.append(b)
        r_p11a, r_p11b = Region("p11a"), Region("p11b")
        XBT = mkbuf("xbt", [pages[8]])
        XBT.ap = view(8 * PAGE, PAGE, BF16).rearrange("p (a b) -> p a b", b=512)
        GV = mkbuf("gv", [pages[8], pages[9]])
        GV.ap = view(8 * PAGE, 2 * PAGE, F32).rearrange("p (a b) -> p a b", b=2048)
        YB = mkbuf("yb", [pages[10]], store=True)
        YB.ap = view(10 * PAGE, PAGE, BF16).rearrange("p (a b) -> p a b", b=2048)
        XS = mkbuf("xs", [r_p11a], store=True)
        XS.ap = view(11 * PAGE, 8192, BF16)
        ABT = mkbuf("abt", [r_p11b], store=True)
        ABT.ap = view(11 * PAGE + 8192, 8192, BF16)

        FB = mkbuf("fb", [pages[6]])
        FB.ap = view(6 * PAGE, PAGE, BF16).rearrange("p (a b) -> p a b", b=512)
        NCL = NCH // 2
        SLP = {}
        for nm, pg in (("p7", 7), ("p8", 8), ("p10", 10)):
            b = mkbuf("slab_" + nm, [pages[pg]], load=True)
            b.ap = view(pg * PAGE, NCL * 2 * 256 * 2, BF16).rearrange("p (a c b) -> p a c b", c=2, b=256)
            SLP[nm] = b
        MT = mkbuf("mt", [pages[7], pages[8]])
        MT.ap = view(7 * PAGE, 2 * PAGE, BF16).rearrange("p (a b) -> p a b", b=512)
        YT = mkbuf("yt", [pages[9]], load=True)
        YT.ap = view(9 * PAGE, PAGE, BF16).rearrange("p (a b) -> p a b", b=2048)
        XR, OT = [], []
        p9 = [Region(f"p9_{i}") for i in range(4)]
        YT.regions = p9
        for i in range(2):
            b = mkbuf(f"xr{i}", [p9[i]], load=True)
            b.ap = view(9 * PAGE + i * 4096, 4096, F32).rearrange("p (a b) -> p a b", b=256)
            XR.append(b)
            b = mkbuf(f"ot{i}", [p9[2 + i]], store=True)
            b.ap = view(9 * PAGE + 8192 + i * 4096, 4096, F32).rearrange("p (a b) -> p a b", b=256)
            OT.append(b)
        AT = mkbuf("at", [pages[10]])
        AT.ap = view(10 * PAGE, PAGE, BF16).rearrange("p (a b) -> p a b", b=512)
        TMP = []
        for i in range(8):
            b = mkbuf(f"tmp{i}", [r_p11a if i < 4 else r_p11b])
            b.ap = view(11 * PAGE + i * 2048, 2048, F32)
            TMP.append(b)
        for i in range(8):
            TMP[i].regions = [Region(f"tmpr{i}")]
        XS.regions = [TMP[i].regions[0] for i in range(4)]
        ABT.regions = [TMP[i].regions[0] for i in range(4, 8)]

        toff = [NPAGE * PAGE]

        def tail(name, nbytes, dt, load=True):
            b = mkbuf(name, [Region(name)], load=load)
            b.ap = view(toff[0], nbytes, dt)
            toff[0] += nbytes
            assert toff[0] <= NPAGE * PAGE + TAIL, (name, toff[0])
            return b

        CS256 = tail("cs256", 2048, BF16)
        CS256.ap = CS256.ap.rearrange("p (a b) -> p a b", b=512)
        WSTB = tail("wstb", 4096, BF16, load=False)
        WSTB.ap = WSTB.ap.rearrange("p (a b) -> p a b", b=128)
        B2 = tail("b2", 8192, F32, load=False)
        B2.ap = B2.ap.rearrange("p (a b) -> p a b", b=128)
        IDB = tail("identb", 256, BF16)
        GT = tail("gt", 128, F32)
        GCOL = tail("gcol", 64, F32)
        BGA = tail("bga", 128, F32)
        BGB = tail("bgb", 128, F32)
        STAT = tail("stat", 256, F32, load=False)
        IDF = mkbuf("identf", [pages[8]], load=True)
        IDF.ap = view(8 * PAGE, 512, F32)
        ONES = mkbuf("ones", [pages[8]], load=True)
        ONES.ap = view(8 * PAGE + 512, 512, F32)

        def stat(c0, c1):
            return STAT.ap[:, c0:c1]

        SN = []
        for i in range(2):
            b = Buf(f"sn{i}", [Region(f"sn{i}")])
            b.ap = STAT.ap[:, 2 * i:2 * i + 2]
            SN.append(b)
        SLNS = [Buf(f"sln{i}", [Region(f"sln{i}")]) for i in range(2)]
        SZ2 = Buf("sz2", p9)
        SZ2.ap = view(9 * PAGE, PAGE, F32).rearrange("p (a b) -> p a b", b=256)

        PS = []
        for i in range(8):
            b = mkbuf(f"bank{i}", [Region(f"bank{i}")])
            b.ap = banks[i][:]
            b.apb = banks[i][:].bitcast(BF16)
            PS.append(b)
        ps_ctr = [0]

        def next_bank():
            b = PS[ps_ctr[0] % 8]
            ps_ctr[0] += 1
            return b

        def dram_buf(name):
            return Buf(name, [Region(name)])

        hT_r, AB_r, y_r, xa_r, xb_r, yout_r = (dram_buf(n) for n in ("hT_s", "AB_s", "y_s", "xa_s", "xb_s", "yout"))

        def mm_unit(bank, mms, reads, cont=False):
            def fn(e, mms=mms):
                ins = None
                for (o, l, r, a, b) in mms:
                    ins = e.matmul(o, l, r, start=a, stop=b)
                return ins

            S.op("pe", fn, reads=reads, writes=[bank], skip_own=cont)

        def load_q(q, idx, ncols, double=False):
            sl = next_slot(double)
            if hasattr(q, "slots"):
                q = q.slots[idx]
            S.dma("sp", sl.flat, q.ap[idx], reads=[q], writes=[sl], sem=sl.lsem)
            sl.ap = sl.flat.rearrange("p (a b) -> p a b", b=ncols)
            return sl

        def run_groups(groups):
            n = len(groups)
            loaded = [None] * n
            c = 0
            for i in range(n):
                while c < n and c - groups[c][2] <= i:
                    loaded[c] = groups[c][0]()
                    c += 1
                groups[i][1](loaded[i])
                loaded[i] = None

        def wcols(wl, c0, n):
            return wl[:, c0:c0 + n].rearrange("(dc p) e -> p dc e", p=128)

        alt = [0]

        def evac_copy(out_ap, in_ap, reads, writes):
            alt[0] ^= 1
            if alt[0]:
                S.op("dve", lambda e: e.tensor_copy(out=out_ap, in_=in_ap), reads=reads, writes=writes)
            else:
                S.op("act", lambda e: e.copy(out=out_ap, in_=in_ap), reads=reads, writes=writes)

        S.dma("sp", CS256.ap, cs256_d, writes=[CS256], sem=CS256.lsem)
        S.dma("sp", IDB.ap, identb_d, writes=[IDB], sem=IDB.lsem)

        def layer_setup(l):
            kw = dict(allow_slow_non_contiguous=True)
            S.dma("sp", IDF.ap, identf_d, writes=[IDF], sem=IDF.lsem)
            S.dma("sp", ONES.ap[0:1, :], ones_d, writes=[ONES], sem=ONES.lsem)
            S.dma("sp", GT.ap, norm_g[l].rearrange("(dc p) -> p dc", p=128), writes=[GT], sem=GT.lsem, **kw)
            S.dma("sp", GCOL.ap, ln_g[l].rearrange("(h c) -> c h", c=128), writes=[GCOL], sem=GCOL.lsem, **kw)
            S.dma("sp", BGA.ap, b_gate[l, 0].rearrange("(e p) -> p e", p=128), writes=[BGA], sem=BGA.lsem, **kw)
            S.dma("sp", BGB.ap, b_gate[l, 1].rearrange("(e p) -> p e", p=128), writes=[BGB], sem=BGB.lsem, **kw)
            WSN = XIN[0]
            wsn = view(6 * PAGE, 8192, F32).rearrange("p (h q) -> p h q", q=128)
            wst32 = view(6 * PAGE + 8192, 8192, F32).rearrange("p (h q) -> p h q", q=128)
            TB = XIN[1]
            brep = view(7 * PAGE, 8192, F32)
            bsrow = view(7 * PAGE + 8192, 8192, F32)
            S.dma("sp", wsn, w_sp[l].rearrange("h p q -> p h q"), writes=[WSN], sem=WSN.lsem)
            S.dma("sp", brep, ln_b[l].partition_broadcast(128), writes=[TB], sem=TB.lsem)
            S.dma("sp", bsrow[0:1, :], b_sp[l:l + 1, :], writes=[TB], sem=TB.lsem)
            for hg in range(4):
                bank = next_bank()

                def fn(e, hg=hg, bank=bank):
                    ins = None
                    for i in range(4):
                        h = hg * 4 + i
                        ins = e.transpose(out=bank.ap[:, i * 128:(i + 1) * 128], in_=wsn[:, h, :], identity=IDF.ap)
                    return ins

                S.op("pe", fn, reads=[WSN, IDF], writes=[bank])
                o32 = wst32[:, hg * 4:(hg + 1) * 4, :]
                ob = WSTB.ap[:, hg * 4:(hg + 1) * 4, :]
                src = bank.ap.rearrange("p (a b) -> p a b", b=128)
                S.op("dve", lambda e, o32=o32, src=src: e.tensor_copy(out=o32, in_=src), reads=[bank], writes=[WSN])
                S.op("dve", lambda e, ob=ob, src=src: e.tensor_copy(out=ob, in_=src), reads=[bank], writes=[WSTB])
            for hg in range(4):
                bank = next_bank()
                mms = []
                for i in range(4):
                    h = hg * 4 + i
                    o = bank.ap[:, i * 128:(i + 1) * 128]
                    mms.append((o, brep[:, h * 128:(h + 1) * 128], wst32[:, h, :], True, False))
                    mms.append((o, ONES.ap[0:1, :], bsrow[0:1, h * 128:(h + 1) * 128], False, True))
                mm_unit(bank, mms, reads=[WSN, TB, ONES])
                ob = B2.ap[:, hg * 4:(hg + 1) * 4, :]
                src = bank.ap.rearrange("p (a b) -> p a b", b=128)
                S.op("dve", lambda e, ob=ob, src=src: e.tensor_copy(out=ob, in_=src), reads=[bank], writes=[B2])

        Q = {}

        def mkq(l, name, nslot, elems):
            q = Buf(f"q_{name}{l}", [Region(f"q_{name}{l}")])
            q.ap = dscr(f"q_{name}{l}", [nslot, 128, elems], BF16)
            q.sem = sem_alloc(f"q_{name}{l}")
            Q[(l, name)] = q
            return q

        for l in range(DEPTH):
            mkq(l, "xb", 16, 4096)
            mkq(l, "v", 8, 8192)
            mkq(l, "zb", 16, 4096)
            mkq(l, "u", 16, 4096)
            mkq(l, "za", 16, 4096)
            mkq(l, "ga", 32, 4096)
            mkq(l, "gb", 32, 4096)
            mkq(l, "wab", 32, 4096)
            mkq(l, "wo", 16, 8192)

        gates = {}

        for nm, ns_ in (("xb", 16), ("v", 8)):
            q0 = Q[(0, nm)]
            q0.slots = []
            for i in range(ns_):
                sb = Buf(f"q_{nm}0_{i}", [Region(f"q_{nm}0_{i}")])
                sb.ap = q0.ap
                sb.sem = sem_alloc(f"q_{nm}0_{i}")
                q0.slots.append(sb)

        def conv(l, name, idx, src, ncols, sub=None, gate=None):
            q = Q[(l, name)]
            if hasattr(q, "slots"):
                q = q.slots[idx]
            dst = q.ap[idx].rearrange("p (a b) -> p a b", b=ncols)
            if sub is not None:
                dst = dst[:, sub[0]:sub[1], :]
            S.dma("pool", dst, src, reads=([gate] if gate is not None else []), writes=[q], sem=q.sem)

        def convert_layer(l, gate_fn):
            wl = w_in[l]
            for j in range(16):
                conv(l, "xb", j, wcols(wl, 6144 + j * 128, 128), 128, gate=gate_fn(0))
            for eg in range(8):
                conv(l, "v", eg, wcols(wl, 2048 + eg * 256, 256), 256, gate=gate_fn(0))
            for j in range(16):
                conv(l, "zb", j, wcols(wl, 8192 + j * 128, 128), 128, gate=gate_fn(1))
            for h in range(16):
                conv(l, "u", h, wcols(wl, h * 128, 128), 128, gate=gate_fn(2))
                conv(l, "za", h, wcols(wl, 4096 + h * 128, 128), 128, gate=gate_fn(2))
            for e in range(32):
                g = gate_fn(3 + e // 8)
                conv(l, "ga", e, wcols(wl, 10240 + e * 128, 128), 128, gate=g)
                conv(l, "wab", e, wcols(w_a[l], e * 128, 128), 128, sub=(0, 16), gate=g)
                conv(l, "wab", e, wcols(w_b[l], e * 128, 128), 128, sub=(16, 32), gate=g)
                conv(l, "gb", e, wcols(wl, 14336 + e * 128, 128), 128, gate=g)
            for eg in range(16):
                conv(l, "wo", eg, wcols(w_out[l], eg * 256, 256), 256, gate=gate_fn(7))

        def pass_a(l, xsrc, xsrc_r):
            groups = []
            XSB = [XS, ABT]

            def xload(t, s):
                xin = XIN[s % 2]
                r0 = t * 512 + s * 128
                S.dma("sp", xin.ap, xsrc[r0:r0 + 128, :], reads=[xsrc_r], writes=[xin], sem=xin.lsem)

            def prologue(t, s):
                xin = XIN[s % 2]
                xs = XSB[s % 2]
                sn = SN[s % 2]
                ssq = sn.ap[:, 0:1]
                rs = sn.ap[:, 1:2]
                S.op("act", lambda e, xin=xin, xs=xs, ssq=ssq: e.activation(out=xs.ap, in_=xin.ap, func=AF.Square, accum_out=ssq),
                     reads=[xin], writes=[xs, sn])
                S.op("act", lambda e, rs=rs, ssq=ssq: e.activation(out=rs, in_=ssq, func=AF.Sqrt, scale=1.0 / D, bias=EPS),
                     reads=[sn], writes=[sn])
                S.op("dve", lambda e, rs=rs: e.reciprocal(out=rs, in_=rs), reads=[sn], writes=[sn])
                S.op("act", lambda e, xin=xin, xs=xs, rs=rs: e.activation(out=xs.ap[:, 0:2048], in_=xin.ap[:, 0:2048], func=AF.Copy, scale=rs),
                     reads=[xin, sn], writes=[xs])
                S.op("dve", lambda e, xin=xin, xs=xs, rs=rs: e.tensor_scalar(out=xs.ap[:, 2048:4096], in0=xin.ap[:, 2048:4096], scalar1=rs, scalar2=None, op0=ALU.mult),
                     reads=[xin, sn], writes=[xs])

            def ln_cols(s):
                c0 = 8 + 28 * (s % 2)
                return stat(c0, c0 + 24), stat(c0 + 24, c0 + 26), stat(c0 + 26, c0 + 27)

            def ln_stage1(s):
                sl_ = SLNS[s % 2]
                bstf, mv, rs2 = ln_cols(s)
                bst = bstf.rearrange("p (a b) -> p a b", b=6)
                for q in range(4):
                    S.op("dve", lambda e, s=s, q=q, bst=bst: e.bn_stats(out=bst[:, q, :], in_=GV.ap[:, s, q * 512:(q + 1) * 512]),
                         reads=[GV], writes=[sl_])
                S.op("dve", lambda e, mv=mv, bstf=bstf: e.bn_aggr(out=mv, in_=bstf), reads=[sl_], writes=[sl_])
                S.op("act", lambda e, mv=mv, rs2=rs2: e.activation(out=rs2, in_=mv[:, 1:2], func=AF.Sqrt, bias=EPS), reads=[sl_], writes=[sl_])

            def ln_stage2(s):
                sl_ = SLNS[s % 2]
                bstf, mv, rs2 = ln_cols(s)
                S.op("dve", lambda e, rs2=rs2: e.reciprocal(out=rs2, in_=rs2), reads=[sl_], writes=[sl_])
                S.op("dve", lambda e, s=s, mv=mv, rs2=rs2: e.tensor_scalar(out=YB.ap[:, s, :], in0=GV.ap[:, s, :], scalar1=mv[:, 0:1], scalar2=rs2,
                                                         op0=ALU.subtract, op1=ALU.mult),
                     reads=[GV, sl_], writes=[YB])

            def ln_store(t):
                S.dma("sp", y_s[t * 512:(t + 1) * 512, :].rearrange("(s p) c -> p s c", p=128), YB.ap,
                      reads=[YB], writes=[y_r], sem=YB.ssem)

            LN_SCHED = {0: [(ln_stage1, 0), (ln_stage1, 1)], 1: [(ln_stage2, 0), (ln_stage1, 2)],
                        2: [(ln_stage2, 1), (ln_stage1, 3)], 3: [(ln_stage2, 2), (ln_stage2, 3)]}

            for t in range(NT):
                def n_phase(_, t=t):
                    if t == 0:
                        xload(0, 0)
                        xload(0, 1)
                        prologue(0, 0)
                        prologue(0, 1)
                    for s in range(4):
                        xs = XSB[s % 2]
                        for dg in range(4):
                            bank = next_bank()

                            def fn(e, dg=dg, bank=bank, xs=xs):
                                ins = None
                                for i in range(8):
                                    dc = dg * 8 + i
                                    ins = e.transpose(out=bank.apb[:, i * 128:(i + 1) * 128],
                                                      in_=xs.ap[:, dc * 128:(dc + 1) * 128], identity=IDB.ap)
                                return ins

                            S.op("pe", fn, reads=[xs, IDB], writes=[bank])
                            o = HT.ap[:, dg * 8:(dg + 1) * 8, s * 128:(s + 1) * 128]
                            src = bank.apb.rearrange("p (a b) -> p a b", b=128)
                            g = GT.ap[:, dg * 8:(dg + 1) * 8].unsqueeze(2).to_broadcast([128, 8, 128])
                            S.op("dve", lambda e, o=o, src=src, g=g: e.tensor_tensor(out=o, in0=src, in1=g, op=ALU.mult),
                                 reads=[bank, GT], writes=[HT])
                        if s + 2 < 4:
                            xload(t, s + 2)
                            prologue(t, s + 2)
                        if t > 0:
                            for f_, a_ in LN_SCHED[s]:
                                f_(a_)
                            if s == 3:
                                ln_store(t - 1)
                    S.dma("sp", hT_s[t], HT.ap, reads=[HT], writes=[hT_r], sem=HT.ssem)

                groups.append((lambda: None, n_phase, 2))
                for j in range(16):
                    def ld(j=j):
                        return load_q(Q[(l, "xb")], j, 128)

                    def cp(sl, j=j):
                        bank = next_bank()
                        mms = [(bank.ap, sl.ap[:, dc, :], HT.ap[:, dc, :], dc == 0, dc == 31) for dc in range(32)]
                        mm_unit(bank, mms, reads=[sl, HT])
                        evac_copy(XBT.ap[:, j, :], bank.ap, reads=[bank], writes=[XBT])

                    groups.append((ld, cp, 2))
                def cd_phase(_, t=t):
                    for s in range(4):
                        abt = XSB[(s + 1) % 2]
                        for g in range(8):
                            bank = next_bank()
                            mms = [(bank.ap, XBT.ap[:, 2 * g + kc, s * 128:(s + 1) * 128], CS256.ap[:, kc, :], kc == 0, kc == 1) for kc in range(2)]
                            mm_unit(bank, mms, reads=[XBT, CS256])
                            evac_copy(abt.ap[:, g * 512:(g + 1) * 512], bank.ap, reads=[bank], writes=[abt])
                        ncx = t * 4 + s
                        S.dma("sp", AB_s[:, :, ncx, :].rearrange("j p e -> p j e"), abt.ap.rearrange("p (j e) -> p j e", e=256),
                              reads=[abt], writes=[AB_r], sem=abt.ssem)

                groups.append((lambda: None, cd_phase, 2))
                for eg in range(8):
                    def ld(eg=eg):
                        return load_q(Q[(l, "v")], eg, 256, double=True)

                    def cp(sl, t=t, eg=eg):
                        for s in range(4):
                            bank = next_bank()
                            o = bank.ap[:, 0:256]
                            mms = [(o, HT.ap[:, dc, s * 128:(s + 1) * 128], sl.ap[:, dc, :], dc == 0, dc == 31) for dc in range(32)]
                            mm_unit(bank, mms, reads=[sl, HT])
                            dst = GV.ap[:, s, eg * 256:(eg + 1) * 256]
                            S.op("act", lambda e, dst=dst, o=o: e.activation(out=dst, in_=o, func=AF.Gelu_apprx_tanh),
                                 reads=[bank], writes=[GV])
                        if eg == 6 and t + 1 < NT:
                            xload(t + 1, 0)
                            xload(t + 1, 1)
                        if eg == 7 and t + 1 < NT:
                            prologue(t + 1, 0)
                            prologue(t + 1, 1)

                    groups.append((ld, cp, 2))

            def ln_last(_):
                ln_stage1(0)
                ln_stage1(1)
                ln_stage2(0)
                ln_stage1(2)
                ln_stage2(1)
                ln_stage1(3)
                ln_stage2(2)
                ln_stage2(3)
                ln_store(NT - 1)

            groups.append((lambda: None, ln_last, 2))
            run_groups(groups)

        def pass_b(l, xsrc, xsrc_r, xdst, xdst_r):
            groups = []

            def load_ht(t):
                S.dma("sp", HT.ap, hT_s[t], reads=[hT_r], writes=[HT], sem=HT.lsem)
                g = Buf("gate", [Region("gate")])
                g.regions[0].writers = {HT.lsem: HT.lsem.count}
                gates[(l, t)] = g

            for t in range(NT):
                for half in range(2):
                    for j in range(16):
                        def ld(j=j, half=half):
                            z = load_q(Q[(l, "zb")], j, 128) if half == 0 else None
                            ab = next_slot(double=True)
                            S.dma("sp", ab.flat[:, 0:NCH * 256], AB_s[j].rearrange("p a b -> p (a b)"), reads=[AB_r], writes=[ab], sem=ab.lsem)
                            ab.ap = ab.flat.rearrange("p (a b) -> p a b", b=256)
                            return (z, ab)

                        def cp(sl, t=t, half=half, j=j):
                            zsl, absl = sl
                            if j == 0:
                                if half == 0 and t == 0:
                                    load_ht(0)
                                if half == 0:
                                    S.dma("sp", SLP["p7"].ap, dft_d[t * 2][:, 0:NCL], reads=[], writes=[SLP["p7"]], sem=SLP["p7"].lsem)
                                    S.dma("sp", SLP["p8"].ap, dft_d[t * 2][:, NCL:NCH], reads=[], writes=[SLP["p8"]], sem=SLP["p8"].lsem)
                                    S.dma("sp", SLP["p10"].ap, dft_d[t * 2 + 1][:, 0:NCL], reads=[], writes=[SLP["p10"]], sem=SLP["p10"].lsem)
                                else:
                                    S.dma("sp", SLP["p7"].ap, dft_d[t * 2 + 1][:, NCL:NCH], reads=[], writes=[SLP["p7"]], sem=SLP["p7"].lsem)
                            slo, shi = (SLP["p7"], SLP["p8"]) if half == 0 else (SLP["p10"], SLP["p7"])
                            if half == 0:
                                bz = next_bank()
                                mms = [(bz.ap, zsl.ap[:, dc, :], HT.ap[:, dc, :], dc == 0, dc == 31) for dc in range(32)]
                                mm_unit(bz, mms, reads=[zsl, HT])
                            bf = next_bank()
                            of = bf.ap[:, 0:256]
                            mms = []
                            for n in range(NCL):
                                mms.append((of, absl.ap[:, n, 0:128], slo.ap[:, n, 0, :], n == 0, False))
                                mms.append((of, absl.ap[:, n, 128:256], slo.ap[:, n, 1, :], False, False))
                            mm_unit(bf, mms, reads=[absl, slo])
                            mms = []
                            for n in range(NCL):
                                mms.append((of, absl.ap[:, NCL + n, 0:128], shi.ap[:, n, 0, :], False, False))
                                mms.append((of, absl.ap[:, NCL + n, 128:256], shi.ap[:, n, 1, :], False, n == NCL - 1))
                            mm_unit(bf, mms, reads=[absl, shi], cont=True)
                            dst = FB.ap[:, j, half * 256:(half + 1) * 256]
                            if half == 0:
                                tm = TMP[j % 2]
                                S.op("act", lambda e, tm=tm, bz=bz: e.activation(out=tm.ap[:, 0:256], in_=bz.ap[:, 0:256], func=AF.Silu),
                                     reads=[bz], writes=[tm])
                                S.op("act", lambda e, bz=bz: e.activation(out=SZ2.ap[:, j, :], in_=bz.ap[:, 256:512], func=AF.Silu),
                                     reads=[bz], writes=[SZ2])
                                S.op("dve", lambda e, dst=dst, of=of, tm=tm: e.tensor_tensor(out=dst, in0=of, in1=tm.ap[:, 0:256], op=ALU.mult),
                                     reads=[bf, tm], writes=[FB])
                            else:
                                S.op("dve", lambda e, dst=dst, of=of: e.tensor_tensor(out=dst, in0=of, in1=SZ2.ap[:, j, :], op=ALU.mult),
                                     reads=[bf, SZ2], writes=[FB])
                                if j == 15:
                                    S.dma("sp", YT.ap, y_s[t * 512:(t + 1) * 512, :].rearrange("(s p) c -> p s c", p=128),
                                          reads=[y_r], writes=[YT], sem=YT.lsem)

                        groups.append((ld, cp, 1))
                for h in range(16):
                    def ld(h=h):
                        return (load_q(Q[(l, "u")], h, 128), load_q(Q[(l, "za")], h, 128))

                    def cp(sl, t=t, h=h):
                        usl, zsl = sl
                        bu = next_bank()
                        mm_unit(bu, [(bu.ap, usl.ap[:, dc, :], HT.ap[:, dc, :], dc == 0, dc == 31) for dc in range(32)], reads=[usl, HT])
                        bz = next_bank()
                        mm_unit(bz, [(bz.ap, zsl.ap[:, dc, :], HT.ap[:, dc, :], dc == 0, dc == 31) for dc in range(32)], reads=[zsl, HT])
                        bs = next_bank()
                        mms = [(bs.ap[:, c * 128:(c + 1) * 128], YT.ap[:, c, h * 128:(h + 1) * 128], WSTB.ap[:, h, :], True, True) for c in range(4)]
                        mm_unit(bs, mms, reads=[YT, WSTB])
                        k = (h % 2) * 4
                        t1, t2, t3 = TMP[k], TMP[k + 1], TMP[k + 2]
                        S.op("act", lambda e, t1=t1, bu=bu: e.activation(out=t1.ap, in_=bu.ap, func=AF.Gelu_apprx_tanh), reads=[bu], writes=[t1])
                        S.op("act", lambda e, t2=t2, bz=bz: e.activation(out=t2.ap, in_=bz.ap, func=AF.Silu), reads=[bz], writes=[t2])
                        S.op("dve", lambda e, t3=t3, bs=bs, h=h: e.scalar_tensor_tensor(
                            out=t3.ap.rearrange("p (a b) -> p a b", b=128), in0=bs.ap.rearrange("p (a b) -> p a b", b=128),
                            scalar=GCOL.ap[:, h:h + 1], in1=B2.ap[:, h:h + 1, :].to_broadcast([128, 4, 128]),
                            op0=ALU.mult, op1=ALU.add), reads=[bs, GCOL, B2], writes=[t3])
                        S.op("dve", lambda e, t1=t1, t2=t2: e.tensor_tensor(out=t1.ap, in0=t1.ap, in1=t2.ap, op=ALU.mult),
                             reads=[t1, t2], writes=[t1])
                        S.op("dve", lambda e, t1=t1, t3=t3, h=h: e.tensor_tensor(out=AT.ap[:, h, :], in0=t3.ap, in1=t1.ap, op=ALU.mult),
                             reads=[t1, t3], writes=[AT])

                    groups.append((ld, cp, 2))
                for ech in range(32):
                    def ld(ech=ech):
                        return (load_q(Q[(l, "ga")], ech, 128), load_q(Q[(l, "wab")], ech, 128), load_q(Q[(l, "gb")], ech, 128))

                    def cp(sl, ech=ech):
                        ga, wab, gb = sl
                        b_ga = next_bank()
                        mm_unit(b_ga, [(b_ga.ap, ga.ap[:, dc, :], HT.ap[:, dc, :], dc == 0, dc == 31) for dc in range(32)], reads=[ga, HT])
                        b_ya = next_bank()
                        mm_unit(b_ya, [(b_ya.ap, wab.ap[:, kc, :], AT.ap[:, kc, :], kc == 0, kc == 15) for kc in range(16)], reads=[wab, AT])
                        b_gb = next_bank()
                        mm_unit(b_gb, [(b_gb.ap, gb.ap[:, dc, :], HT.ap[:, dc, :], dc == 0, dc == 31) for dc in range(32)], reads=[gb, HT])
                        b_yb = next_bank()
                        mm_unit(b_yb, [(b_yb.ap, wab.ap[:, 16 + kc, :], FB.ap[:, kc, :], kc == 0, kc == 15) for kc in range(16)], reads=[wab, FB])
                        k = (ech % 2) * 4
                        ta, tb = TMP[k], TMP[k + 1]
                        S.op("act", lambda e, ta=ta, b_ga=b_ga: e.activation(out=ta.ap, in_=b_ga.ap, func=AF.Sigmoid, bias=BGA.ap[:, ech:ech + 1]),
                             reads=[b_ga, BGA], writes=[ta])
                        S.op("act", lambda e, tb=tb, b_gb=b_gb: e.activation(out=tb.ap, in_=b_gb.ap, func=AF.Sigmoid, bias=BGB.ap[:, ech:ech + 1]),
                             reads=[b_gb, BGB], writes=[tb])
                        S.op("dve", lambda e, ta=ta, b_ya=b_ya: e.tensor_tensor(out=ta.ap, in0=b_ya.ap, in1=ta.ap, op=ALU.mult),
                             reads=[b_ya, ta], writes=[ta])
                        S.op("dve", lambda e, tb=tb, b_yb=b_yb: e.tensor_tensor(out=tb.ap, in0=b_yb.ap, in1=tb.ap, op=ALU.mult),
                             reads=[b_yb, tb], writes=[tb])
                        S.op("dve", lambda e, ta=ta, tb=tb: e.tensor_tensor(out=MT.ap[:, ech, :], in0=ta.ap, in1=tb.ap, op=ALU.add),
                             reads=[ta, tb], writes=[MT])

                    groups.append((ld, cp, 1))
                for eg in range(16):
                    def ld(eg=eg):
                        return load_q(Q[(l, "wo")], eg, 256, double=True)

                    def cp(wo, t=t, eg=eg):
                        xr, ot = XR[eg % 2], OT[eg % 2]
                        if eg == 1 and t + 1 < NT:
                            load_ht(t + 1)
                        S.dma("sp", xr.ap, xsrc[t * 512:(t + 1) * 512, eg * 256:(eg + 1) * 256].rearrange("(s p) e -> p s e", p=128),
                              reads=[xsrc_r], writes=[xr], sem=xr.lsem)
                        for s in range(4):
                            bank = next_bank()
                            o = bank.ap[:, 0:256]
                            mm_unit(bank, [(o, MT.ap[:, dc, s * 128:(s + 1) * 128], wo.ap[:, dc, :], dc == 0, dc == 31) for dc in range(32)],
                                    reads=[wo, MT])
                            S.op("dve", lambda e, ot=ot, xr=xr, o=o, s=s: e.tensor_tensor(out=ot.ap[:, s, :], in0=o, in1=xr.ap[:, s, :], op=ALU.add),
                                 reads=[bank, xr], writes=[ot])
                        S.dma("sp", xdst[t * 512:(t + 1) * 512, eg * 256:(eg + 1) * 256].rearrange("(s p) e -> p s e", p=128), ot.ap,
                              reads=[ot], writes=[xdst_r], sem=ot.ssem)

                    groups.append((ld, cp, 2))
            run_groups(groups)

        def pass_c(xsrc, xsrc_r):
            FG = HT
            fg = view(4 * PAGE, PAGE, F32)
            S.dma("sp", fg, final_g.partition_broadcast(128), reads=[], writes=[FG], sem=FG.lsem)
            XC = []
            for i in range(6):
                bb = mkbuf(f"xc{i}", [pages[6 + i]] if i < 5 else [TMP[k].regions[0] for k in range(8)], load=True, store=True)
                bb.ap = view((6 + i) * PAGE, PAGE, F32)
                XC.append(bb)
            JK = Buf("junkc", slot_regs[0:1])
            jk = view(0, 8192, BF16)
            for r in range(TOK // 128):
                xin = XC[r % 6]
                sn = SN[r % 2]
                ssq = sn.ap[:, 0:1]
                rs = sn.ap[:, 1:2]
                S.dma("sp", xin.ap, xsrc[r * 128:(r + 1) * 128, :], reads=[xsrc_r], writes=[xin], sem=xin.lsem)
                S.op("act", lambda e, xin=xin, ssq=ssq: e.activation(out=jk, in_=xin.ap, func=AF.Square, accum_out=ssq),
                     reads=[xin], writes=[JK, sn])
                S.op("act", lambda e, rs=rs, ssq=ssq: e.activation(out=rs, in_=ssq, func=AF.Sqrt, scale=1.0 / D, bias=EPS), reads=[sn], writes=[sn])
                S.op("dve", lambda e, rs=rs: e.reciprocal(out=rs, in_=rs), reads=[sn], writes=[sn])
                S.op("dve", lambda e, xin=xin, rs=rs: e.scalar_tensor_tensor(out=xin.ap, in0=xin.ap, scalar=rs, in1=fg, op0=ALU.mult, op1=ALU.mult),
                     reads=[xin, sn, FG], writes=[xin])
                S.dma("sp", y_out[r * 128:(r + 1) * 128, :], xin.ap, reads=[xin], writes=[yout_r], sem=xin.ssem)

        xsrc, xsrc_r = x_in, dram_buf("x_in")
        dsts = [(xa_s, xa_r), (xb_s, xb_r)]
        convert_layer(0, lambda k: None)
        for l in range(DEPTH):
            layer_setup(l)
            pass_a(l, xsrc, xsrc_r)
            xdst, xdst_r = dsts[l % 2]
            pass_b(l, xsrc, xsrc_r, xdst, xdst_r)
            if l + 1 < DEPTH:
                convert_layer(l + 1, lambda k, l=l: gates[(l, min(k, NT - 1))])
            xsrc, xsrc_r = xdst, xdst_r
        pass_c(xsrc, xsrc_r)
        S.final_wait("sp", [yout_r])

        with nc.Block() as block:
            @block.tensor
            def _(e):
                Sched.run(S.engs["pe"], e)

            @block.scalar
            def _(e):
                Sched.run(S.engs["act"], e)

            @block.vector
            def _(e):
                Sched.run(S.engs["dve"], e)

            @block.gpsimd
            def _(e):
                Sched.run(S.engs["pool"], e)

            @block.sync
            def _(e):
                Sched.run(S.engs["sp"], e)
    return nc


def make_cs256():
    c = np.arange(256)[:, None]
    j = np.arange(256)[None, :]
    ang = 2.0 * np.pi * ((c * j) % 256) / 256.0
    C = np.cos(ang) / 16.0
    Sn = np.sin(ang) / 16.0
    full = np.concatenate([C[:, 0:128], Sn[:, 0:128], C[:, 128:256], Sn[:, 128:256]], axis=1)
    out = full.reshape(2, 128, 512).transpose(1, 0, 2)
    return np.ascontiguousarray(out).astype(ml_dtypes.bfloat16)


def make_dft(TOK, SEQ):
    NCH = TOK // 128
    NHALF = TOK // 256
    n = np.arange(TOK)
    out = np.zeros((NHALF, 128, NCH, 2, 256), dtype=ml_dtypes.bfloat16)
    scale = 1.0 / np.sqrt(SEQ)
    for hh in range(NHALF):
        k = hh * 256 + np.arange(256)
        same = (n[:, None] // SEQ) == (k[None, :] // SEQ)
        m = ((n[:, None] % SEQ) * (k[None, :] % SEQ)) % SEQ
        ang = 2.0 * np.pi * m / SEQ
        Cm = np.where(same, np.cos(ang), 0.0) * scale
        Sm = np.where(same, -np.sin(ang), 0.0) * scale
        out[hh, :, :, 0, :] = Cm.reshape(NCH, 128, 256).transpose(1, 0, 2).astype(ml_dtypes.bfloat16)
        out[hh, :, :, 1, :] = Sm.reshape(NCH, 128, 256).transpose(1, 0, 2).astype(ml_dtypes.bfloat16)
    return out


_CACHE = {}


def run_cores(x_list, seqs, weights, DEPTH):
    TOK = x_list[0].shape[0]
    key = (TOK, DEPTH)
    if key not in _CACHE:
        _CACHE[key] = build_program(TOK, DEPTH)
    nc = _CACHE[key]
    cs256 = make_cs256()
    dfts = {}
    for sq in set(seqs):
        dfts[sq] = make_dft(TOK, sq)
    identb = np.eye(128, dtype=np.float32).astype(ml_dtypes.bfloat16)
    identf = np.eye(128, dtype=np.float32)
    ones = np.ones((1, 128), dtype=np.float32)
    in_maps = []
    for xc, sq in zip(x_list, seqs):
        m = dict(weights)
        m["x"] = np.ascontiguousarray(xc)
        m["cs256"] = cs256
        m["dft"] = dfts[sq]
        m["identb"] = identb
        m["identf"] = identf
        m["ones"] = ones
        in_maps.append(m)
    res = run_bass_kernel_spmd(nc, in_maps, core_ids=list(range(len(x_list))))
    return [np.asarray(r["y"]) for r in res.results]


def kernel(x_prompt, x_sample, norm_g, w_in, sgu_ln_g, sgu_ln_b, w_spatial, b_spatial, w_a, w_b, b_gate, w_out, final_g):
    f = lambda a: np.ascontiguousarray(np.asarray(a, dtype=np.float32))
    x_prompt, x_sample = f(x_prompt), f(x_sample)
    DEPTH = int(np.asarray(norm_g).shape[0])
    weights = {
        "w_in": f(w_in), "w_a": f(w_a), "w_b": f(w_b), "w_out": f(w_out), "norm_g": f(norm_g),
        "sgu_ln_g": f(sgu_ln_g), "sgu_ln_b": f(sgu_ln_b), "w_spatial": f(w_spatial),
        "b_spatial": f(b_spatial).reshape(DEPTH, 16 * 128), "b_gate": f(b_gate), "final_g": f(final_g),
    }
    B, SQ, _ = x_prompt.shape
    B2_, SQ2, _ = x_sample.shape
    x_list, seqs = [], []
    for c in range(4):
        x_list.append(x_prompt[2 * c:2 * c + 2].reshape(2 * SQ, D))
        seqs.append(SQ)
    for c in range(4):
        x_list.append(x_sample[c].reshape(SQ2, D))
        seqs.append(SQ2)
    outs = run_cores(x_list, seqs, weights, DEPTH)
    y_prompt = np.stack([outs[c].reshape(2, SQ, D) for c in range(4)], axis=0).reshape(B, SQ, D)
    y_sample = np.stack([outs[4 + c].reshape(SQ2, D) for c in range(4)], axis=0)
    return (y_prompt.astype(np.float32), y_sample.astype(np.float32))
```
